# Optimizing a Trainium2 kernel written in Bass

```python
import jax, jax.numpy as jnp
from jax import lax
import numpy as np

D_MODEL = 2048
BATCH = 8
SEQ = 2048
DEPTH = 1

HEAD_DIM = 128
D_MIX = D_MODEL
A_HEADS = (D_MIX // 2) // HEAD_DIM
A_KV_HEADS = 2
A_WINDOW = 128
A_BLOCK = 128
B_HEADS = (D_MIX // 2) // HEAD_DIM
DILATED_PATTERNS = ((128, 1), (512, 4), (2048, 16))
B_BLOCK = 64
ROT_DIM = HEAD_DIM // 4
ROPE_THETA = 500000.0
D_FF = 4 * D_MODEL
PLE_DIM = 256
ALPHA = (2.0 * DEPTH) ** 0.25
BETA = (8.0 * DEPTH) ** -0.25
LN_EPS = 1e-5
RMS_EPS = 1e-6
NEG_INF = -1e30

A_Q = A_HEADS * HEAD_DIM
A_KV = A_KV_HEADS * HEAD_DIM
B_QKV = B_HEADS * HEAD_DIM
D_IN = A_Q + 2 * A_KV + 3 * B_QKV

kernel_name = "hybrid_window_gqa_dilated_attn_deepnorm"


def layer_norm(x, g, b):
    xf = x.astype(jnp.float32)
    mu = jnp.mean(xf, axis=-1, keepdims=True)
    var = jnp.mean(jnp.square(xf - mu), axis=-1, keepdims=True)
    y = (xf - mu) * lax.rsqrt(var + LN_EPS) * g.astype(jnp.float32) + b.astype(jnp.float32)
    return y.astype(x.dtype)


def rms_norm(x, g):
    xf = x.astype(jnp.float32)
    y = xf * lax.rsqrt(jnp.mean(jnp.square(xf), axis=-1, keepdims=True) + RMS_EPS) * g.astype(jnp.float32)
    return y.astype(x.dtype)


def rope_partial(t, cos, sin):
    half = ROT_DIM // 2
    t1 = t[..., :half].astype(jnp.float32)
    t2 = t[..., half:ROT_DIM].astype(jnp.float32)
    rot = jnp.concatenate([t1 * cos - t2 * sin, t2 * cos + t1 * sin], axis=-1)
    return jnp.concatenate([rot.astype(t.dtype), t[..., ROT_DIM:]], axis=-1)


def banded_attention(q, k, v, halo, block, sink=None):
    bt, seq_len, hq, d = q.shape
    hkv = k.shape[2]
    grp = hq // hkv
    nb = -(-seq_len // block)
    lp = nb * block
    kw = block + 2 * halo
    qb = jnp.pad(q, ((0, 0), (0, lp - seq_len), (0, 0), (0, 0))).reshape(bt, nb, block, hkv, grp, d)
    pad_kv = ((0, 0), (halo, halo + lp - seq_len), (0, 0), (0, 0))
    key_idx = (jnp.arange(nb) * block)[:, None] + jnp.arange(kw)[None, :]
    kb = jnp.pad(k, pad_kv)[:, key_idx]
    vb = jnp.pad(v, pad_kv)[:, key_idx]
    s = jnp.einsum('bnqhgd,bnkhd->bnhgqk', qb, kb,
                   preferred_element_type=jnp.float32) * (d ** -0.5)
    qpos = (jnp.arange(nb) * block)[:, None] + jnp.arange(block)[None, :]
    kpos = key_idx - halo
    mask = ((jnp.abs(qpos[:, :, None] - kpos[:, None, :]) <= halo)
            & (kpos >= 0)[:, None, :] & (kpos < seq_len)[:, None, :])
    s = jnp.where(mask[None, :, None, None], s, NEG_INF)
    m = jnp.max(s, axis=-1)
    if sink is not None:
        sk = sink.astype(jnp.float32).reshape(hkv, grp)[None, None, :, :, None]
        m = jnp.maximum(m, sk)
    e = jnp.exp(s - m[..., None])
    denom = jnp.sum(e, axis=-1)
    if sink is not None:
        denom = denom + jnp.exp(sk - m)
    o = jnp.einsum('bnhgqk,bnkhd->bnqhgd', e, vb.astype(jnp.float32))
    o = o / jnp.transpose(denom, (0, 1, 4, 2, 3))[..., None]
    lse = jnp.transpose(m + jnp.log(denom), (0, 1, 4, 2, 3)).reshape(bt, lp, hq)[:, :seq_len]
    o = o.reshape(bt, lp, hq, d)[:, :seq_len].astype(q.dtype)
    return o, lse


def dilated_attention(q, k, v):
    b, s, h, d = q.shape
    outs, lses = [], []
    for window, rate in DILATED_PATTERNS:
        halo = window // (2 * rate)
        sub = s // rate

        def to_res(t):
            return jnp.transpose(t.reshape(b, sub, rate, h, d), (0, 2, 1, 3, 4)).reshape(b * rate, sub, h, d)

        o, lse = banded_attention(to_res(q), to_res(k), to_res(v), halo, B_BLOCK)
        outs.append(jnp.transpose(o.reshape(b, rate, sub, h, d), (0, 2, 1, 3, 4)).reshape(b, s, h, d))
        lses.append(jnp.transpose(lse.reshape(b, rate, sub, h), (0, 2, 1, 3)).reshape(b, s, h))
    w = jax.nn.softmax(jnp.stack(lses, axis=0), axis=0)
    out = jnp.sum(w[..., None] * jnp.stack(outs, axis=0).astype(jnp.float32), axis=0)
    return out.astype(q.dtype)


def setup_inputs(seed: int = 0) -> dict:
    key = jax.random.key(seed)
    ks = jax.random.split(key, 20)
    f32 = jnp.float32
    x = jax.random.normal(ks[0], (BATCH, SEQ, D_MODEL), f32)
    p = jax.random.normal(ks[1], (DEPTH, BATCH, SEQ, PLE_DIM), f32)
    positions = (jnp.arange(SEQ, dtype=jnp.int32)[None, :]
                 + jax.random.randint(ks[2], (BATCH, 1), 0, 512, dtype=jnp.int32))
    col_scale = jnp.concatenate([
        jnp.ones((A_Q + A_KV,), f32), jnp.full((A_KV,), BETA, f32),
        jnp.ones((2 * B_QKV,), f32), jnp.full((B_QKV,), BETA, f32)])
    w_in = jax.random.normal(ks[3], (DEPTH, D_MODEL, D_IN), f32) * (D_MODEL ** -0.5) * col_scale
    sink_a = 0.5 * jax.random.normal(ks[4], (DEPTH, A_HEADS), f32)
    gn_a = 1.0 + 0.02 * jax.random.normal(ks[5], (DEPTH, A_Q), f32)
    gn_b = 1.0 + 0.02 * jax.random.normal(ks[6], (DEPTH, B_QKV), f32)
    w_o = jax.random.normal(ks[7], (DEPTH, D_MIX, D_MODEL), f32) * (D_MIX ** -0.5) * BETA
    ln1_g = 1.0 + 0.02 * jax.random.normal(ks[8], (DEPTH, D_MODEL), f32)
    ln1_b = 0.02 * jax.random.normal(ks[9], (DEPTH, D_MODEL), f32)
    w1 = jax.random.normal(ks[10], (DEPTH, D_MODEL, D_FF), f32) * (D_MODEL ** -0.5) * BETA
    w2 = jax.random.normal(ks[11], (DEPTH, D_FF, D_MODEL), f32) * (D_FF ** -0.5) * BETA
    w_ple = jax.random.normal(ks[12], (DEPTH, PLE_DIM, D_MODEL), f32) * (PLE_DIM ** -0.5) * BETA
    w_ple_gate = jax.random.normal(ks[13], (DEPTH, D_MODEL, D_MODEL), f32) * (D_MODEL ** -0.5)
    ln2_g = 1.0 + 0.02 * jax.random.normal(ks[14], (DEPTH, D_MODEL), f32)
    ln2_b = 0.02 * jax.random.normal(ks[15], (DEPTH, D_MODEL), f32)
    return {"x": x, "p": p, "positions": positions, "w_in": w_in, "sink_a": sink_a,
            "gn_a": gn_a, "gn_b": gn_b, "w_o": w_o, "ln1_g": ln1_g, "ln1_b": ln1_b,
            "w1": w1, "w2": w2, "w_ple": w_ple, "w_ple_gate": w_ple_gate,
            "ln2_g": ln2_g, "ln2_b": ln2_b}


def reference(x, p, positions, w_in, sink_a, gn_a, gn_b, w_o, ln1_g, ln1_b,
              w1, w2, w_ple, w_ple_gate, ln2_g, ln2_b):
    b, s, _ = x.shape
    inv_freq = ROPE_THETA ** (-jnp.arange(0, ROT_DIM, 2, dtype=jnp.float32) / ROT_DIM)
    ang = positions.astype(jnp.float32)[..., None] * inv_freq
    cos = jnp.cos(ang)[:, :, None, :]
    sin = jnp.sin(ang)[:, :, None, :]
    split_at = [A_Q, A_Q + A_KV, A_Q + 2 * A_KV, A_Q + 2 * A_KV + B_QKV, A_Q + 2 * A_KV + 2 * B_QKV]
    h = x
    for i in range(DEPTH):
        proj = h @ w_in[i]
        qa, ka, va, qb, kb, vb = jnp.split(proj, split_at, axis=-1)
        qa = rope_partial(qa.reshape(b, s, A_HEADS, HEAD_DIM), cos, sin)
        ka = rope_partial(ka.reshape(b, s, A_KV_HEADS, HEAD_DIM), cos, sin)
        va = va.reshape(b, s, A_KV_HEADS, HEAD_DIM)
        qb = rope_partial(qb.reshape(b, s, B_HEADS, HEAD_DIM), cos, sin)
        kb = rope_partial(kb.reshape(b, s, B_HEADS, HEAD_DIM), cos, sin)
        vb = vb.reshape(b, s, B_HEADS, HEAD_DIM)
        oa, _ = banded_attention(qa, ka, va, A_WINDOW, A_BLOCK, sink_a[i])
        ob = dilated_attention(qb, kb, vb)
        ya = rms_norm(oa.reshape(b, s, A_Q), gn_a[i])
        yb = rms_norm(ob.reshape(b, s, B_QKV), gn_b[i])
        mix = jnp.concatenate([ya, yb], axis=-1) @ w_o[i]
        h = layer_norm(ALPHA * h + mix, ln1_g[i], ln1_b[i])
        ff = jnp.square(jax.nn.relu(h @ w1[i])) @ w2[i]
        ple = (p[i] @ w_ple[i]) * jax.nn.sigmoid(h @ w_ple_gate[i])
        h = layer_norm(ALPHA * h + ff + ple, ln2_g[i], ln2_b[i])
    return h
```

```python
import numpy as np
import concourse.bass as bass
import concourse.mybir as mybir
from concourse.bass_utils import run_bass_kernel_spmd

F32 = mybir.dt.float32
BF16 = mybir.dt.bfloat16
I32 = mybir.dt.int32
AF = mybir.ActivationFunctionType
ALU = mybir.AluOpType

S = 2048
D = 2048
DFF = 8192
NT = 16
ALPHA = 2.0 ** 0.25
LN_EPS = 1e-5
RMS_EPS = 1e-6
SCALE = 128.0 ** -0.5
ENGS = ["pe", "act", "dve", "pool", "sp"]
RDMA = 8


class Op:
    __slots__ = ("eng", "fn", "dma", "deps", "needed", "count", "dk")


class Prog:
    def __init__(self):
        self.ops = {e: [] for e in ENGS}
        self.lastw = {}
        self.readers = {}

    def add(self, eng, fn, reads=(), writes=(), dma=False):
        op = Op()
        op.eng, op.fn, op.dma, op.deps, op.needed, op.count, op.dk = eng, fn, dma, [], False, None, None

        def dep(d, kind):
            if d is op:
                return
            if (not d.dma) and (not dma) and d.eng == eng:
                if eng == "pe":
                    return
            if d not in op.deps:
                op.deps.append(d)
                d.needed = True

        for k in reads:
            w = self.lastw.get(k)
            if w is not None:
                dep(w, "RAW")
        for k in writes:
            w = self.lastw.get(k)
            if w is not None:
                dep(w, "WAW")
            for r in self.readers.get(k, {}).values():
                dep(r, "WAR")
        for k in writes:
            self.lastw[k] = op
            self.readers[k] = {}
        for k in reads:
            rd = self.readers.setdefault(k, {})
            if dma:
                rd[("dma", len(self.ops[eng]))] = op
            else:
                rd[eng] = op
        self.ops[eng].append(op)
        return op

    def alias(self, new_keys, old_keys):
        rd = {}
        for ok in old_keys:
            w = self.lastw.get(ok)
            if w is not None:
                rd[("w", ok)] = w
            for kk, r in self.readers.get(ok, {}).items():
                rd[(ok, kk)] = r
        for nk in new_keys:
            self.lastw.pop(nk, None)
            self.readers[nk] = dict(rd)

    def emit(self, nc, block, sems, dsems):
        for e in ENGS:
            c = 0
            k = 0
            for op in self.ops[e]:
                if op.dma:
                    op.dk = k
                    k += 1
                elif op.needed:
                    c += 1
                    op.count = c

        def dsem_of(op):
            return dsems[op.eng][op.dk % RDMA], 16 * (op.dk // RDMA + 1)

        def run(h, eng):
            waited = {}
            for op in self.ops[eng]:
                ws = []
                for d in op.deps:
                    if d.dma:
                        ws.append(dsem_of(d))
                    else:
                        ws.append((sems[d.eng], d.count))
                if op.dma and op.dk >= RDMA:
                    ws.append((dsems[eng][op.dk % RDMA], 16 * (op.dk // RDMA)))
                for s, v in ws:
                    if waited.get(id(s), 0) < v:
                        h.wait_ge(s, v)
                        waited[id(s)] = v
                ins = op.fn(h)
                if op.dma:
                    ins.then_inc(dsems[eng][op.dk % RDMA], 16)
                elif op.needed:
                    ins.then_inc(sems[eng], 1)
            nd = sum(1 for op in self.ops[eng] if op.dma)
            for j in range(min(RDMA, nd)):
                tot = (nd - 1 - j) // RDMA + 1
                if waited.get(id(dsems[eng][j]), 0) < 16 * tot:
                    h.wait_ge(dsems[eng][j], 16 * tot)

        @block.tensor
        def _(h):
            run(h, "pe")

        @block.scalar
        def _(h):
            run(h, "act")

        @block.vector
        def _(h):
            run(h, "dve")

        @block.gpsimd
        def _(h):
            run(h, "pool")

        @block.sync
        def _(h):
            run(h, "sp")


class StopBuild(Exception):
    pass


def build_nc(stop=None):
    nc = bass.Bass("TRN2", target_bir_lowering=False)
    P = Prog()

    def din(name, shape, dt=F32):
        return nc.dram_tensor(name, list(shape), dt, kind="ExternalInput").ap()

    x_d = din("x", [S, D])
    p_d = din("p", [S, 256])
    pos_d = din("pos", [1, S], I32)
    win_d = din("w_in", [D, 4608])
    wo_d = din("w_o", [D, D])
    w1_d = din("w1", [D, DFF])
    w2_d = din("w2", [DFF, D])
    wple_d = din("w_ple", [256, D])
    wg_d = din("w_g", [D, D])
    lnp_d = din("lnp", [4, D])
    sink_d = din("sink", [1, 8])
    gnT_d = din("gnT", [128, 16])
    ident_d = din("ident", [128, 128])
    maskA_d = din("maskA", [128, 9 * 128])
    maskB_d = din("maskB", [128, 23 * 128])
    rc_d = din("ropec", [48, 2])
    out_d = nc.dram_tensor("out", [S, D], F32, kind="ExternalOutput").ap()
    h1s_d = nc.dram_tensor("h1s", [S, D], F32, kind="Internal").ap()
    h1Ts_d = nc.dram_tensor("h1Ts", [128, 16 * 1024], BF16, kind="Internal").ap()
    dbg_d = nc.dram_tensor("dbg", [128, 32768], F32, kind="ExternalOutput").ap() if stop is not None else None

    OT_O = 0
    XT_O = 65536
    SLAB_O = 131072
    NSLAB = 6
    R_O = SLAB_O + NSLAB * 4096
    R_SZ = 38400
    C_O = R_O + R_SZ
    C_SZ = 12288
    AW = (C_O + C_SZ) // 4

    from contextlib import ExitStack
    es = ExitStack()
    arena = es.enter_context(nc.sbuf_tensor("arena", [128, AW], F32))
    ps = [es.enter_context(nc.psum_tensor(f"ps{i}", [128, 512], F32)) for i in range(8)]
    sems = {e: es.enter_context(nc.semaphore(f"sem_{e}")) for e in ENGS}
    dsems = {e: [es.enter_context(nc.semaphore(f"dsem_{e}{j}")) for j in range(RDMA)] for e in ["pool", "sp", "act"]}

    def f32v(off, n):
        assert off % 4 == 0
        return arena[:, off // 4: off // 4 + n]

    def bfv(off, n):
        assert off % 4 == 0 and n % 2 == 0
        return arena[:, off // 4: off // 4 + n // 2].bitcast(BF16)

    def psb(i):
        return ps[i][:, :].bitcast(BF16)

    OT = bfv(OT_O, 16 * 2048).rearrange("p (k t) -> p k t", k=16)
    accT = f32v(OT_O, 16 * 1024).rearrange("p (k t) -> p k t", k=16)
    xb = [bfv(OT_O + i * 4096, 2048) for i in range(2)]
    posi = arena[:, (OT_O + 8192) // 4: (OT_O + 8192) // 4 + 2048].bitcast(I32)
    posf = f32v(OT_O + 16384, 2048)
    xT = bfv(XT_O, 16 * 2048).rearrange("p (k t) -> p k t", k=16)
    bufA = bfv(XT_O, 16 * 1024).rearrange("p (k t) -> p k t", k=16)
    ybuf = [f32v(XT_O + 32768 + i * 8192, 2048) for i in range(4)]
    hidT = bfv(XT_O + 32768, 8 * 1024).rearrange("p (k t) -> p k t", k=8)
    ytile = f32v(XT_O + 32768 + 16384, 2048)
    resid = f32v(XT_O + 32768 + 24576, 2048)
    slabs = [bfv(SLAB_O + i * 4096, 2048).rearrange("p (k j) -> p k j", k=16) for i in range(NSLAB)]
    QT = bfv(R_O, 2048)
    KT = bfv(R_O + 4096, 2048)
    Vt = bfv(R_O + 8192, 2048).rearrange("p (t d) -> p t d", t=16)
    tabC = f32v(R_O + 12288, 2048)
    tabS = f32v(R_O + 20480, 2048)
    PT = [bfv(R_O + 28672 + i * 1024, 512) for i in range(3)]
    On = [bfv(R_O + 31744 + i * 1024, 512).rearrange("p (c d) -> p c d", c=4) for i in range(2)]
    ropeP = f32v(R_O + 33792, 512)
    ropeS = f32v(R_O + 35840, 512)
    lng = f32v(R_O, 2048)
    lnb = f32v(R_O + 8192, 2048)
    hb = bfv(R_O + 16384, 2048)
    stg = [bfv(R_O + 20480 + i * 4096, 2048).rearrange("p (k t) -> p k t", k=16) for i in range(2)]
    tmpA = [f32v(R_O + 28672 + i * 512, 128) for i in range(2)]
    tmpB = [f32v(R_O + 29696 + i * 512, 128) for i in range(2)]
    pT = bfv(R_O + 16384, 2048).rearrange("p (k t) -> p k t", k=2)
    rtmp = [f32v(R_O + 20480 + i * 2048, 512) for i in range(2)]
    sgt = [f32v(R_O + 24576 + i * 2048, 512) for i in range(2)]
    pbuf = [bfv(R_O + 28672 + i * 512, 256) for i in range(2)]
    co = [C_O]

    def calloc(nbytes):
        o = co[0]
        co[0] += (nbytes + 3) // 4 * 4
        assert co[0] <= C_O + C_SZ
        return o
    identb = bfv(calloc(256), 128)
    identf = f32v(calloc(512), 128)
    maskA = bfv(calloc(9 * 256), 9 * 128)
    maskB = bfv(calloc(23 * 256), 23 * 128)
    gnT = f32v(calloc(64), 16)
    sinkb = f32v(calloc(32), 8)
    esink = f32v(calloc(32), 8)
    ropec = f32v(calloc(8), 2)
    ssq = f32v(calloc(1024), 256).rearrange("p (m t h) -> p m t h", m=2, t=16)
    ssum = f32v(calloc(128), 32).rearrange("p (m t) -> p m t", m=2)
    rstd = f32v(calloc(128), 32).rearrange("p (m t) -> p m t", m=2)
    rden = [f32v(calloc(16), 4) for _ in range(2)]
    stats = f32v(calloc(96), 24).rearrange("p (a b) -> p a b", b=3)
    mv = f32v(calloc(8), 2)
    rs = f32v(calloc(4), 1)
    onesb = bfv(calloc(4), 2)
    sqj = bfv(calloc(256), 128)

    def dma(eng, out, in_, reads, writes):
        return P.add(eng, lambda h, out=out, in_=in_: h.dma_start(out=out, in_=in_), reads, writes, dma=True)

    def mm(out, lhsT, rhs, start, stop, reads, writes, sgc=False):
        return P.add("pe", lambda h, a=(out, lhsT, rhs, start, stop, sgc): h.matmul(a[0], a[1], a[2], start=a[3], stop=a[4], skip_group_check=a[5]),
                     reads, writes)

    def tr(out, in_, ident, reads, writes):
        return P.add("pe", lambda h, a=(out, in_, ident): h.transpose(a[0], a[1], a[2]), reads, writes)

    def act(out, in_, func, reads, writes, scale=None, accum_out=None):
        kw = {}
        if scale is not None:
            kw["scale"] = scale
        if accum_out is not None:
            kw["accum_out"] = accum_out
        return P.add("act", lambda h, a=(out, in_, func), kw=kw: h.activation(out=a[0], in_=a[1], func=a[2], **kw),
                     reads, writes)

    def ts(eng, out, in0, s1, s2, op0, op1, reads, writes):
        if op1 is None:
            return P.add(eng, lambda h, a=(out, in0, s1, op0): h.tensor_scalar(out=a[0], in0=a[1], scalar1=a[2], scalar2=None, op0=a[3]),
                         reads, writes)
        return P.add(eng, lambda h, a=(out, in0, s1, s2, op0, op1): h.tensor_scalar(out=a[0], in0=a[1], scalar1=a[2], scalar2=a[3], op0=a[4], op1=a[5]),
                     reads, writes)

    def tt(eng, out, in0, in1, op, reads, writes):
        return P.add(eng, lambda h, a=(out, in0, in1, op): h.tensor_tensor(out=a[0], in0=a[1], in1=a[2], op=a[3]), reads, writes)

    def stt(eng, out, in0, scalar, in1, op0, op1, reads, writes):
        return P.add(eng, lambda h, a=(out, in0, scalar, in1, op0, op1): h.scalar_tensor_tensor(out=a[0], in0=a[1], scalar=a[2], in1=a[3], op0=a[4], op1=a[5]),
                     reads, writes)

    def cp(eng, out, in_, reads, writes):
        return P.add(eng, lambda h, a=(out, in_): h.tensor_copy(out=a[0], in_=a[1]), reads, writes)

    def memset(eng, ap, val, writes):
        return P.add(eng, lambda h, a=(ap, val): h.memset(a[0], a[1]), (), writes)

    slab_ctr = [0]

    def load_slab(src_ap, nk=16):
        i = slab_ctr[0] % NSLAB
        slab_ctr[0] += 1
        dma("pool", slabs[i][:, 0:nk, :], src_ap.rearrange("(k p) j -> p k j", p=128), (), [("slab", i)])
        return i

    def dump(level, views):
        if stop != level:
            return
        allk = [k for k, w in P.lastw.items() if w is not None]
        o = 0
        for v in views:
            n = int(np.prod(v.shape[1:]))
            if len(v.shape) == 3:
                for a_ in range(v.shape[1]):
                    for c0 in range(0, v.shape[2], 512):
                        c1 = min(c0 + 512, v.shape[2])
                        dma("pool", dbg_d[0:v.shape[0], o + a_ * v.shape[2] + c0:o + a_ * v.shape[2] + c1], v[:, a_, c0:c1], allk, [("dbg", o, a_, c0)])
            else:
                for c0 in range(0, n, 512):
                    c1 = min(c0 + 512, n)
                    dma("pool", dbg_d[0:v.shape[0], o + c0:o + c1], v[:, c0:c1], allk, [("dbg", o, c0)])
            o += n
        raise StopBuild()

    try:
        dma("sp", identf, ident_d, (), ["identf"])
        dma("pool", identb, ident_d, (), ["identb"])
        dma("pool", maskA, maskA_d, (), ["maskA"])
        dma("pool", maskB, maskB_d, (), ["maskB"])
        dma("sp", gnT, gnT_d, (), ["gnT"])
        dma("sp", sinkb, sink_d.partition_broadcast(128), (), ["sinkb"])
        dma("sp", ropec[0:48, :], rc_d, (), ["ropec"])
        dma("sp", posi[0:48, :], pos_d.partition_broadcast(48), (), ["posi"])
        act(esink, sinkb, AF.Exp, ["sinkb"], ["esink"])
        memset("dve", onesb, 1.0, ["onesb"])
        memset("dve", ssq, 0.0, ["ssq"])
        memset("dve", ropeS[0:48, :], 0.0, ["ropeS"])
        MAGIC = 12582912.0
        TWO_PI = float(np.float32(2.0 * np.pi))
        PI_SAFE = 3.1415925
        cp("dve", posf[0:48, :], posi[0:48, :], ["posi"], ["posf"])
        ts("dve", tabS[0:48, :], posf[0:48, :], ropec[0:48, 0:1], None, ALU.mult, None, ["posf", "ropec"], ["tabS"])

        def range_reduce(src_key):
            ts("dve", posf[0:48, :], tabS[0:48, :], 1.0 / TWO_PI, MAGIC, ALU.mult, ALU.add, [src_key], ["posf"])
            ts("dve", posf[0:48, :], posf[0:48, :], MAGIC, -TWO_PI, ALU.subtract, ALU.mult, ["posf"], ["posf"])
            tt("dve", posf[0:48, :], posf[0:48, :], tabS[0:48, :], ALU.add, ["posf", src_key], ["posf"])
            ts("dve", posf[0:48, :], posf[0:48, :], PI_SAFE, -PI_SAFE, ALU.min, ALU.max, ["posf"], ["posf"])
        range_reduce("tabS")
        act(tabC[0:48, :], posf[0:48, :], AF.Sin, ["posf"], ["tabC"])
        ts("dve", tabC[0:48, :], tabC[0:48, :], ropec[0:48, 1:2], None, ALU.mult, None, ["tabC", "ropec"], ["tabC"])
        ts("dve", tabS[0:48, :], tabS[0:48, :], float(np.float32(np.pi / 2)), None, ALU.add, None, ["tabS"], ["tabS"])
        range_reduce("tabS")
        act(tabS[0:48, :], posf[0:48, :], AF.Sin, ["posf"], ["tabS"])
        cosx, sinx = tabS, tabC
        dump(-1, [cosx[0:48, :], sinx[0:48, :], esink])

        for b in range(NT):
            j = b % 2
            for q4 in range(4):
                dma("pool", xb[j][:, q4 * 512:(q4 + 1) * 512], x_d[b * 128:(b + 1) * 128, q4 * 512:(q4 + 1) * 512], (),
                    [("xb", j, q4)])
            for kc in range(16):
                bank = 2 * j + kc // 8
                tr(psb(bank)[:, (kc % 8) * 128:(kc % 8 + 1) * 128], xb[j][:, kc * 128:(kc + 1) * 128], identb,
                   [("xb", j, kc // 4), "identb"], [("ps", bank)])
            for hf in range(2):
                bank = 2 * j + hf
                src = psb(bank)[:, 0:1024].rearrange("p (k t) -> p k t", k=8)
                dst = xT[:, hf * 8:(hf + 1) * 8, b * 128:(b + 1) * 128]
                if hf == 0:
                    act(dst, src, AF.Copy, [("ps", bank)], [("xT", b // 4)])
                else:
                    cp("dve", dst, src, [("ps", bank)], [("xT", b // 4)])

        dump(0, [xT[:, 0:4, :], cosx[0:48, :], sinx[0:48, :]])
        xT_keys = [("xT", g) for g in range(4)]
        P.alias([("OT", h, g) for h in range(16) for g in range(4)], [("xb", j_, q_) for j_ in range(2) for q_ in range(4)] + ["posi", "posf"])

        proj_ctr = [0]

        def proj_unit(u, dest, dkey, rope):
            si = load_slab(win_d[:, u * 128:(u + 1) * 128])
            for tg in range(4):
                bank = proj_ctr[0] % 2
                proj_ctr[0] += 1
                for kc in range(16):
                    mm(ps[bank][:, :], slabs[si][:, kc, :], xT[:, kc, tg * 512:(tg + 1) * 512], kc == 0, kc == 15,
                       [("slab", si), ("xT", tg)], [("ps", bank)])
                cs = slice(tg * 512, (tg + 1) * 512)
                act(dest[:, cs], ps[bank][:, :], AF.Copy, [("ps", bank)], [(dkey, tg)])
                if rope:
                    act(ropeP[0:48, :], ps[bank][0:48, :], AF.Copy, [("ps", bank)], ["ropeP"])
                    tt("dve", ropeS[0:16, :], ropeP[32:48, :], sinx[32:48, cs], ALU.mult, ["ropeP", "tabC"], ["ropeS"])
                    tt("dve", ropeS[32:48, :], ropeP[0:16, :], sinx[0:16, cs], ALU.mult, ["ropeP", "tabC"], ["ropeS"])
                    tt("dve", ropeP[0:48, :], ropeP[0:48, :], cosx[0:48, cs], ALU.mult, ["ropeP", "tabS"], ["ropeP"])
                    tt("dve", dest[0:48, cs], ropeP[0:48, :], ropeS[0:48, :], ALU.add, ["ropeP", "ropeS"], [(dkey, tg)])

        def v_unit(u):
            si = load_slab(win_d[:, u * 128:(u + 1) * 128])
            for tq in range(4):
                bank = proj_ctr[0] % 2
                proj_ctr[0] += 1
                for i in range(4):
                    b = tq * 4 + i
                    for kc in range(16):
                        mm(ps[bank][:, i * 128:(i + 1) * 128], xT[:, kc, b * 128:(b + 1) * 128], slabs[si][:, kc, :],
                           kc == 0, kc == 15, [("slab", si), ("xT", b // 4)], [("ps", bank)])
                act(Vt[:, tq * 4:(tq + 1) * 4, :], ps[bank][:, :].rearrange("p (t d) -> p t d", t=4), AF.Copy,
                    [("ps", bank)], [("V", tq)])

        item_ctr = [0]
        grp_ctr = [0]

        def attention(mask, mkey, OFF, W, sink_col, hidx, m, hl):
            items = []
            for G in range(4):
                tl = [t for t in range(4 * G - W, 4 * G + 3 + W + 1) if 0 <= t < NT]
                for t in tl:
                    items.append((G, t, t == tl[0], t == tl[-1]))

            def emit_qk(G, t):
                n = item_ctr[0]
                item_ctr[0] += 1
                sb = 2 + n % 2
                pi = n % 3
                mm(ps[sb][:, :], KT[:, t * 128:(t + 1) * 128], QT[:, G * 512:(G + 1) * 512], True, True,
                   [("KT", t // 4), ("QT", G)], [("ps", sb)])
                act(PT[pi], ps[sb][:, :], AF.Exp, [("ps", sb)], [("PT", pi)], scale=SCALE)
                s0 = 4 * G - t + OFF
                tt("dve", PT[pi], PT[pi], mask[:, s0 * 128:(s0 + 4) * 128], ALU.mult, [("PT", pi), mkey], [("PT", pi)])
                return pi

            started = set()

            def emit_pv(G, t, pi, gp):
                for c in range(4):
                    d = 4 * G + c - t
                    if abs(d) > W:
                        continue
                    first = G not in started
                    started.add(G)
                    mm(ps[4 + gp][:, c * 128:(c + 1) * 128], PT[pi][:, c * 128:(c + 1) * 128], Vt[:, t, :], first, False,
                       [("PT", pi), ("V", t // 4)], [("psO", gp)], sgc=True)
                    mm(ps[6][:, 4 * gp + c:4 * gp + c + 1], PT[pi][:, c * 128:(c + 1) * 128], onesb[:, 0:1], first, False,
                       [("PT", pi), "onesb"], [("psD", gp)], sgc=True)

            def emit_fin(G, gp):
                rd = rden[gp]
                if sink_col is not None:
                    ts("dve", rd, ps[6][:, 4 * gp:4 * gp + 4], esink[:, sink_col:sink_col + 1], None, ALU.add, None,
                       [("psD", gp), "esink"], [("rden", gp)])
                    P.add("dve", lambda h, a=rd: h.reciprocal(out=a, in_=a), [("rden", gp)], [("rden", gp)])
                else:
                    P.add("dve", lambda h, a=rd, b_=ps[6][:, 4 * gp:4 * gp + 4]: h.reciprocal(out=a, in_=b_),
                          [("psD", gp)], [("rden", gp)])
                for c in range(4):
                    act(On[gp][:, c, :], ps[4 + gp][:, c * 128:(c + 1) * 128], AF.Copy, [("psO", gp), ("rden", gp)],
                        [("On", gp)], scale=rd[:, c:c + 1])
                for c in range(4):
                    act(sqj, ps[4 + gp][:, c * 128:(c + 1) * 128], AF.Square,
                        [("psO", gp), ("rden", gp)], ["sqjunk", "ssq"], scale=rd[:, c:c + 1],
                        accum_out=ssq[:, m, 4 * G + c, hl:hl + 1])
                for c in range(4):
                    tr(psb(7)[:, gp * 512 + c * 128: gp * 512 + (c + 1) * 128], On[gp][:, c, :], identb,
                       [("On", gp), "identb"], [("psT", gp)])
                ts("dve", OT[:, hidx, G * 512:(G + 1) * 512], psb(7)[:, gp * 512:(gp + 1) * 512], gnT[:, hidx:hidx + 1], None,
                   ALU.mult, None, [("psT", gp), "gnT"], [("OT", hidx, G)])

            prev = None
            gp_of = {}
            for (G, t, first, last) in items:
                if first:
                    gp_of[G] = grp_ctr[0] % 2
                    grp_ctr[0] += 1
                pi = emit_qk(G, t)
                if prev is not None:
                    pG, pt, ppi, plast = prev
                    emit_pv(pG, pt, ppi, gp_of[pG])
                    if plast:
                        emit_fin(pG, gp_of[pG])
                prev = (G, t, pi, last)
            pG, pt, ppi, plast = prev
            emit_pv(pG, pt, ppi, gp_of[pG])
            emit_fin(pG, gp_of[pG])

        def attn_A(hq, kvg):
            attention(maskA, "maskA", 4, 1, hq, hq, 0, hq)

        def attn_B(h):
            attention(maskB, "maskB", 11, 8, None, 8 + h, 1, h)


        for g in range(2):
            proj_unit(8 + g, KT, "KT", True)
            dump(10, [KT])
            v_unit(10 + g)
            dump(11, [KT, Vt])
            for j in range(4):
                hq = 4 * g + j
                proj_unit(hq, QT, "QT", True)
                dump(1, [QT, KT, Vt])
                attn_A(hq, g)
                dump(2, [OT[:, 0, :], ssq[:, 0, :, :]])
        for h in range(8):
            proj_unit(12 + h, QT, "QT", True)
            proj_unit(20 + h, KT, "KT", True)
            v_unit(28 + h)
            attn_B(h)

        for m in range(2):
            P.add("dve", lambda h, a=ssum[:, m, :], b_=ssq[:, m, :, :]: h.reduce_sum(out=a, in_=b_, axis=mybir.AxisListType.X),
                  ["ssq"], ["ssum"])
            ts("dve", rstd[:, m, :], ssum[:, m, :], 1.0 / 1024.0, RMS_EPS, ALU.mult, ALU.add, ["ssum"], ["rstd"])
            act(rstd[:, m, :], rstd[:, m, :], AF.Sqrt, ["rstd"], ["rstd"])
            P.add("dve", lambda h, a=rstd[:, m, :]: h.reciprocal(out=a, in_=a), ["rstd"], ["rstd"])

        dump(3, [OT[:, 7:9, :], OT[:, 15, :], rstd[:, :, :]])
        P.alias(["lng", "lnb", "hb", ("stg", 0), ("stg", 1), ("tmpA", 0), ("tmpA", 1), ("tmpB", 0), ("tmpB", 1)],
                [("QT", g) for g in range(4)] + [("KT", g) for g in range(4)] + [("V", g) for g in range(4)]
                + ["tabC", "tabS", ("PT", 0), ("PT", 1), ("PT", 2), ("On", 0), ("On", 1), "ropeP", "ropeS"])
        P.alias([("bufA", g) for g in range(2)] + [("ybuf", i) for i in range(4)], xT_keys)
        dma("sp", lng, lnp_d[0:1, :].partition_broadcast(128), (), ["lng"])
        dma("sp", lnb, lnp_d[1:2, :].partition_broadcast(128), (), ["lnb"])

        def layer_norm(yt, ykey):
            for j in range(4):
                P.add("dve", lambda h, a=stats[:, 2 * j:2 * j + 2, :], b_=yt[:, j * 512:(j + 1) * 512]: h.bn_stats(a, b_),
                      [ykey], ["stats"])
            P.add("dve", lambda h, a=mv, b_=stats[:, 0:8, :]: h.bn_aggr(a, b_), ["stats"], ["mv"])
            ts("dve", rs, mv[:, 1:2], LN_EPS, None, ALU.add, None, ["mv"], ["rs"])
            act(rs, rs, AF.Sqrt, ["rs"], ["rs"])
            P.add("dve", lambda h, a=rs: h.reciprocal(out=a, in_=a), ["rs"], ["rs"])
            ts("dve", yt, yt, mv[:, 0:1], rs, ALU.subtract, ALU.mult, [ykey, "mv", "rs"], [ykey])
            tt("pool", yt, yt, lng, ALU.mult, [ykey, "lng"], [ykey])
            tt("dve", yt, yt, lnb, ALU.add, [ykey, "lnb"], [ykey])

        tctr = [0]
        for tg2 in range(4):
            for i in range(4):
                b = 4 * tg2 + i
                dma("sp", ybuf[i], x_d[b * 128:(b + 1) * 128, :], (), [("ybuf", i)])
            for cc in range(16):
                si = load_slab(wo_d[:, cc * 128:(cc + 1) * 128])
                bA = 2 * (cc % 2)
                bB = bA + 1
                for i in range(4):
                    b = 4 * tg2 + i
                    for h in range(8):
                        mm(ps[bA][:, i * 128:(i + 1) * 128], OT[:, h, b * 128:(b + 1) * 128], slabs[si][:, h, :], h == 0, h == 7,
                           [("slab", si), ("OT", h, b // 4)], [("ps", bA)])
                    for h in range(8, 16):
                        mm(ps[bB][:, i * 128:(i + 1) * 128], OT[:, h, b * 128:(b + 1) * 128], slabs[si][:, h, :], h == 8, h == 15,
                           [("slab", si), ("OT", h, b // 4)], [("ps", bB)])
                for i in range(4):
                    b = 4 * tg2 + i
                    k = tctr[0] % 2
                    tctr[0] += 1
                    ysl = ybuf[i][:, cc * 128:(cc + 1) * 128]
                    act(tmpA[k], ps[bA][:, i * 128:(i + 1) * 128], AF.Copy, [("ps", bA), "rstd"], [("tmpA", k)],
                        scale=rstd[:, 0, b:b + 1])
                    stt("dve", tmpB[k], ps[bB][:, i * 128:(i + 1) * 128], rstd[:, 1, b:b + 1], tmpA[k], ALU.mult, ALU.add,
                        [("ps", bB), "rstd", ("tmpA", k)], [("tmpB", k)])
                    stt("dve", ysl, ysl, ALPHA, tmpB[k], ALU.mult, ALU.add, [("ybuf", i), ("tmpB", k)], [("ybuf", i)])
            for i in range(4):
                b = 4 * tg2 + i
                layer_norm(ybuf[i], ("ybuf", i))
                dma("sp", h1s_d[b * 128:(b + 1) * 128, :], ybuf[i], [("ybuf", i)], [("h1s", b)])
                act(hb, ybuf[i], AF.Copy, [("ybuf", i)], ["hb"])
                bk0 = 4 + 2 * (b % 2)
                for kc in range(16):
                    bank = bk0 + kc // 8
                    tr(psb(bank)[:, (kc % 8) * 128:(kc % 8 + 1) * 128], hb[:, kc * 128:(kc + 1) * 128], identb,
                       ["hb", "identb"], [("ps", bank)])
                for hf in range(2):
                    bank = bk0 + hf
                    src = psb(bank)[:, 0:1024].rearrange("p (k t) -> p k t", k=8)
                    if b < 8:
                        dst = bufA[:, hf * 8:(hf + 1) * 8, b * 128:(b + 1) * 128]
                        dk = ("bufA", 0)
                    else:
                        dst = stg[b % 2][:, hf * 8:(hf + 1) * 8, :]
                        dk = ("stg", b % 2)
                    if hf == 0:
                        act(dst, src, AF.Copy, [("ps", bank)], [dk])
                    else:
                        cp("dve", dst, src, [("ps", bank)], [dk])
                if b >= 8:
                    dstd = h1Ts_d.rearrange("p (k t) -> p k t", k=16)[:, :, (b - 8) * 128:(b - 7) * 128]
                    dma("sp", dstd, stg[b % 2], [("stg", b % 2)], [("h1Ts", b)])

        dump(4, [bufA[:, 0:2, :], ybuf[3]])
        P.alias([("acc", cc) for cc in range(16)], [("OT", h, g) for h in range(16) for g in range(4)])
        P.alias(["hidT", "ytile", "resid"], [("ybuf", i) for i in range(4)])
        P.alias(["pT", ("rtmp", 0), ("rtmp", 1), ("sgt", 0), ("sgt", 1), ("pbuf", 0), ("pbuf", 1)],
                ["hb", ("stg", 0), ("stg", 1), ("tmpA", 0), ("tmpA", 1), ("tmpB", 0), ("tmpB", 1)])
        dma("sp", lng, lnp_d[2:3, :].partition_broadcast(128), (), ["lng"])
        dma("sp", lnb, lnp_d[3:4, :].partition_broadcast(128), (), ["lnb"])

        hctr = [0]
        octr = [0]
        fctr = [0]
        for g in range(2):
            if g == 1:
                for kc in range(16):
                    dma("sp", bufA[:, kc, :], h1Ts_d[:, kc * 1024:(kc + 1) * 1024],
                        [("h1Ts", b_) for b_ in range(8, 16)] + [("bufA", 0)], [("bufAk", kc)])
            for i in range(8):
                b = 8 * g + i
                k = i % 2
                dma("pool", pbuf[k], p_d[b * 128:(b + 1) * 128, :], (), [("pbuf", k)])
                bank = 6 + k
                for kc in range(2):
                    tr(psb(bank)[:, kc * 128:(kc + 1) * 128], pbuf[k][:, kc * 128:(kc + 1) * 128], identb,
                       [("pbuf", k), "identb"], [("ps", bank)])
                cp("dve", pT[:, :, i * 128:(i + 1) * 128], psb(bank)[:, 0:256].rearrange("p (k t) -> p k t", k=2),
                   [("ps", bank)], ["pT"])
            for cc in range(16):
                sg_ = load_slab(wg_d[:, cc * 128:(cc + 1) * 128])
                sp_ = load_slab(wple_d[:, cc * 128:(cc + 1) * 128], nk=2)
                for hf in range(2):
                    cs = slice(hf * 512, (hf + 1) * 512)
                    bg = hf
                    bp = 2 + hf
                    for kc in range(16):
                        mm(ps[bg][:, :], slabs[sg_][:, kc, :], bufA[:, kc, cs], kc == 0, kc == 15,
                           [("slab", sg_), ("bufA", 0), ("bufAk", kc)], [("ps", bg)])
                    for kc in range(2):
                        mm(ps[bp][:, :], slabs[sp_][:, kc, :], pT[:, kc, cs], kc == 0, kc == 1,
                           [("slab", sp_), "pT"], [("ps", bp)])
                    act(sgt[hf], ps[bg][:, :], AF.Sigmoid, [("ps", bg)], [("sgt", hf)])
                    tt("dve", accT[:, cc, cs], ps[bp][:, :], sgt[hf], ALU.mult, [("ps", bp), ("sgt", hf)], [("acc", cc)])
            for q in range(8):
                for fl in range(8):
                    f = 8 * q + fl
                    si = load_slab(w1_d[:, f * 128:(f + 1) * 128])
                    for hf in range(2):
                        cs = slice(hf * 512, (hf + 1) * 512)
                        bank = hctr[0] % 2
                        hctr[0] += 1
                        for kc in range(16):
                            mm(ps[bank][:, :], slabs[si][:, kc, :], bufA[:, kc, cs], kc == 0, kc == 15,
                               [("slab", si), ("bufA", 0), ("bufAk", kc)], [("ps", bank)])
                        act(rtmp[bank], ps[bank][:, :], AF.Relu, [("ps", bank)], [("rtmp", bank)])
                        tt("dve", hidT[:, fl, cs], rtmp[bank], rtmp[bank], ALU.mult, [("rtmp", bank)], ["hidT"])
                for cc in range(16):
                    si = load_slab(w2_d[q * 1024:(q + 1) * 1024, cc * 128:(cc + 1) * 128], nk=8)
                    for hf in range(2):
                        cs = slice(hf * 512, (hf + 1) * 512)
                        bank = 2 + octr[0] % 4
                        octr[0] += 1
                        for kc in range(8):
                            mm(ps[bank][:, :], slabs[si][:, kc, :], hidT[:, kc, cs], kc == 0, kc == 7,
                               [("slab", si), "hidT"], [("ps", bank)])
                        tt("dve", accT[:, cc, cs], accT[:, cc, cs], ps[bank][:, :], ALU.add, [("acc", cc), ("ps", bank)],
                           [("acc", cc)])
            for i in range(8):
                b = 8 * g + i
                dma("sp", resid, h1s_d[b * 128:(b + 1) * 128, :], [("h1s", b)], ["resid"])
                for c4 in range(4):
                    bank = 6 + fctr[0] % 2
                    fctr[0] += 1
                    for j in range(4):
                        cc = 4 * c4 + j
                        tr(ps[bank][:, j * 128:(j + 1) * 128], accT[:, cc, i * 128:(i + 1) * 128], identf,
                           [("acc", cc), "identf"], [("ps", bank)])
                    cs = slice(c4 * 512, (c4 + 1) * 512)
                    stt("dve", ytile[:, cs], resid[:, cs], ALPHA, ps[bank][:, :], ALU.mult, ALU.add,
                        ["resid", ("ps", bank)], ["ytile"])
                layer_norm(ytile, "ytile")
                dma("sp", out_d[b * 128:(b + 1) * 128, :], ytile, ["ytile"], [("out", b)])

    except StopBuild:
        pass
    with nc.Block() as block:
        P.emit(nc, block, sems, dsems)
    es.close()
    return nc


def _masks():
    k = np.arange(128)[:, None]
    q = np.arange(128)[None, :]
    mA = np.zeros((128, 9, 128), np.float32)
    for s in range(9):
        d = s - 4
        diff = 128 * d + q - k
        mA[:, s, :] = (np.abs(diff) <= 128) & (abs(d) <= 1)
    mB = np.zeros((128, 23, 128), np.float32)
    for s in range(23):
        d = s - 11
        if abs(d) > 8:
            continue
        diff = 128 * d + q - k
        m1 = (np.abs(diff) <= 64)
        m2 = (np.abs(diff) <= 256) & (diff % 4 == 0)
        m3 = (np.abs(diff) <= 1024) & (diff % 16 == 0)
        mB[:, s, :] = m1.astype(np.float32) + m2 + m3
    return mA.reshape(128, -1), mB.reshape(128, -1)


def _head_perm():
    hp = np.concatenate([np.arange(0, 16), np.arange(32, 48), np.arange(16, 32), np.arange(48, 128)])
    perm = np.arange(4608)
    for u in list(range(0, 10)) + list(range(12, 28)):
        perm[u * 128:(u + 1) * 128] = u * 128 + hp
    return perm


_NC_CACHE = {}


def kernel(x, p, positions, w_in, sink_a, gn_a, gn_b, w_o, ln1_g, ln1_b, w1, w2, w_ple, w_ple_gate, ln2_g, ln2_b):
    x = np.asarray(x, np.float32)
    p = np.asarray(p, np.float32)
    positions = np.asarray(positions, np.int32)
    n = 8
    if "nc" not in _NC_CACHE:
        _NC_CACHE["nc"] = build_nc()
    nc = _NC_CACHE["nc"]
    mA, mB = _masks()
    w_in_p = np.ascontiguousarray(np.asarray(w_in, np.float32)[0][:, _head_perm()])
    inv_freq = (np.float32(500000.0) ** (-np.arange(0, 32, 2, dtype=np.float32) / np.float32(32))).astype(np.float32)
    ropec = np.zeros((48, 2), np.float32)
    ropec[0:16, 0] = inv_freq
    ropec[32:48, 0] = inv_freq
    ropec[0:16, 1] = 1.0
    ropec[16:32, 1] = 1.0
    ropec[32:48, 1] = -1.0
    gn = np.concatenate([np.asarray(gn_a, np.float32)[0], np.asarray(gn_b, np.float32)[0]])
    gnT = np.ascontiguousarray(gn.reshape(16, 128).T)
    lnp = np.ascontiguousarray(np.stack([np.asarray(a, np.float32)[0] for a in (ln1_g, ln1_b, ln2_g, ln2_b)]))
    shared = {
        "w_in": w_in_p,
        "w_o": np.ascontiguousarray(np.asarray(w_o, np.float32)[0]),
        "w1": np.ascontiguousarray(np.asarray(w1, np.float32)[0]),
        "w2": np.ascontiguousarray(np.asarray(w2, np.float32)[0]),
        "w_ple": np.ascontiguousarray(np.asarray(w_ple, np.float32)[0]),
        "w_g": np.ascontiguousarray(np.asarray(w_ple_gate, np.float32)[0]),
        "lnp": lnp,
        "sink": np.ascontiguousarray(np.asarray(sink_a, np.float32).reshape(1, 8)),
        "gnT": gnT,
        "ident": np.eye(128, dtype=np.float32),
        "maskA": mA,
        "maskB": mB,
        "ropec": ropec,
    }
    in_maps = []
    for b in range(n):
        d = dict(shared)
        d["x"] = np.ascontiguousarray(x[b])
        d["p"] = np.ascontiguousarray(p[0, b])
        d["pos"] = np.ascontiguousarray(positions[b].reshape(1, S))
        in_maps.append(d)
    res = run_bass_kernel_spmd(nc, in_maps, core_ids=list(range(n)))
    return np.stack([np.asarray(r["out"], np.float32) for r in res.results], axis=0)
```
